# Optimizing a Trainium2 kernel written in Bass

```python
import math
import jax, jax.numpy as jnp
from jax import lax
import numpy as np

D_MODEL = 1024
BATCH = 8
SEQ = 8192
DEPTH = 2

GRID_W = 64
CTX_LEN = 256
N_BRANCH = 3
BRANCH_W = 1024
Q_BLOCK = 128
EPS = 1e-6
ROPE_BASE = 10000.0
NEG_INF = -1e30

MLA_HEADS = 8
MLA_NOPE = 64
MLA_ROPE = 32
MLA_V = 128
MLA_Q_RANK = 384
MLA_KV_RANK = 256

DIFF_HEADS = 8
DIFF_HD = 64
DIFF_V = 2 * DIFF_HD

SWA_HEADS = 16
SWA_KV_HEADS = 4
SWA_GROUP = SWA_HEADS // SWA_KV_HEADS
SWA_HD = 64
WINDOW = 128

IN_SPLITS = (MLA_Q_RANK, MLA_KV_RANK, MLA_ROPE,
             DIFF_HEADS * 2 * DIFF_HD, DIFF_HEADS * 2 * DIFF_HD, DIFF_HEADS * DIFF_V,
             SWA_HEADS * SWA_HD, SWA_KV_HEADS * SWA_HD, SWA_KV_HEADS * SWA_HD,
             N_BRANCH * BRANCH_W, N_BRANCH * D_MODEL)
D_IN = sum(IN_SPLITS)

kernel_name = 'hybrid_mla_diff_swa_dit'


def rms_norm(x, g):
    xf = x.astype(jnp.float32)
    y = xf * lax.rsqrt(jnp.mean(xf * xf, axis=-1, keepdims=True) + EPS)
    return (y * g.astype(jnp.float32)).astype(x.dtype)


def split_cols(p):
    idx = np.cumsum(np.array(IN_SPLITS))[:-1].tolist()
    return jnp.split(p, idx, axis=-1)


def axial_rope_tables(row, col, rot_dim):
    n_freq = rot_dim // 4
    inv_freq = ROPE_BASE ** (-jnp.arange(n_freq, dtype=jnp.float32) / n_freq)
    ang = jnp.concatenate([row.astype(jnp.float32)[:, None] * inv_freq,
                           col.astype(jnp.float32)[:, None] * inv_freq], axis=-1)
    return jnp.cos(ang), jnp.sin(ang)


def apply_rope(x, cos, sin):
    half = x.shape[-1] // 2
    x1, x2 = x[..., :half], x[..., half:]
    cs = cos[:, None, :].astype(x.dtype)
    sn = sin[:, None, :].astype(x.dtype)
    return jnp.concatenate([x1 * cs - x2 * sn, x2 * cs + x1 * sn], axis=-1)


def sweep_query_blocks(fn, *qs):
    b, n = qs[0].shape[:2]
    nb = n // Q_BLOCK
    blocked = tuple(jnp.moveaxis(q.reshape(b, nb, Q_BLOCK, *q.shape[2:]), 1, 0) for q in qs)
    out = lax.map(lambda args: fn(*args), (jnp.arange(nb),) + blocked)
    out = jnp.moveaxis(out, 0, 1)
    return out.reshape(b, n, *out.shape[3:])


def dense_attend(q, k, v):
    s = jnp.einsum('bqhd,bkhd->bhqk', q * (q.shape[-1] ** -0.5), k)
    p = jax.nn.softmax(s.astype(jnp.float32), axis=-1).astype(v.dtype)
    return jnp.einsum('bhqk,bkhd->bqhd', p, v)


def diff_attend(q, k, v, lam):
    s = jnp.einsum('bqhmd,bkhmd->bhmqk', q * (DIFF_HD ** -0.5), k)
    p = jax.nn.softmax(s.astype(jnp.float32), axis=-1)
    a = (p[:, :, 0] - lam * p[:, :, 1]).astype(v.dtype)
    return jnp.einsum('bhqk,bkhd->bqhd', a, v)


def diff_post(o, subln, lam_init):
    b, n = o.shape[:2]
    return (rms_norm(o, subln) * (1.0 - lam_init)).reshape(b, n, DIFF_HEADS * DIFF_V)


def sink_softmax(sink, parts):
    b, h, g, q = parts[0].shape[:4]
    sk = jnp.broadcast_to(sink.astype(jnp.float32).reshape(1, h, g, 1, 1), (b, h, g, q, 1))
    p = jax.nn.softmax(jnp.concatenate([sk] + [s.astype(jnp.float32) for s in parts], axis=-1), axis=-1)
    out, off = [], 1
    for s in parts:
        out.append(p[..., off:off + s.shape[-1]])
        off += s.shape[-1]
    return out


def mla_heads(qc, kvc, kr, q_norm, w_uq, kv_norm, w_ukv, rope):
    b, n = qc.shape[:2]
    q = (rms_norm(qc, q_norm) @ w_uq).reshape(b, n, MLA_HEADS, MLA_NOPE + MLA_ROPE)
    kv = (rms_norm(kvc, kv_norm) @ w_ukv).reshape(b, n, MLA_HEADS, MLA_NOPE + MLA_V)
    q_nope, q_rot = q[..., :MLA_NOPE], q[..., MLA_NOPE:]
    k_nope, v = kv[..., :MLA_NOPE], kv[..., MLA_NOPE:]
    k_rot = kr[:, :, None, :]
    if rope is not None:
        q_rot = apply_rope(q_rot, *rope)
        k_rot = apply_rope(k_rot, *rope)
    k_rot = jnp.broadcast_to(k_rot, (b, n, MLA_HEADS, MLA_ROPE))
    return (jnp.concatenate([q_nope, q_rot], axis=-1), jnp.concatenate([k_nope, k_rot], axis=-1), v)


def diff_heads(q, k, v, rope):
    b, n = q.shape[:2]
    q = q.reshape(b, n, 2 * DIFF_HEADS, DIFF_HD)
    k = k.reshape(b, n, 2 * DIFF_HEADS, DIFF_HD)
    if rope is not None:
        q = apply_rope(q, *rope)
        k = apply_rope(k, *rope)
    return (q.reshape(b, n, DIFF_HEADS, 2, DIFF_HD), k.reshape(b, n, DIFF_HEADS, 2, DIFF_HD),
            v.reshape(b, n, DIFF_HEADS, DIFF_V))


def swa_heads(q, k, v, rope):
    b, n = q.shape[:2]
    q = q.reshape(b, n, SWA_HEADS, SWA_HD)
    k = k.reshape(b, n, SWA_KV_HEADS, SWA_HD)
    v = v.reshape(b, n, SWA_KV_HEADS, SWA_HD)
    if rope is not None:
        q = apply_rope(q, *rope)
        k = apply_rope(k, *rope)
    return q, k, v


def swa_latent(q, k, v, k_ctx, v_ctx, sink):
    b, n_seq = q.shape[:2]
    span = Q_BLOCK + 2 * WINDOW
    pad = ((0, 0), (WINDOW, WINDOW), (0, 0), (0, 0))
    kp, vp = jnp.pad(k, pad), jnp.pad(v, pad)

    def block(i, qb):
        start = i * Q_BLOCK
        kb = lax.dynamic_slice_in_dim(kp, start, span, axis=1)
        vb = lax.dynamic_slice_in_dim(vp, start, span, axis=1)
        qpos = start + jnp.arange(Q_BLOCK)
        kpos = start - WINDOW + jnp.arange(span)
        mask = (jnp.abs(qpos[:, None] - kpos[None, :]) <= WINDOW) & (kpos >= 0)[None, :] & (kpos < n_seq)[None, :]
        qg = qb.reshape(b, Q_BLOCK, SWA_KV_HEADS, SWA_GROUP, SWA_HD) * (SWA_HD ** -0.5)
        s_loc = jnp.where(mask, jnp.einsum('bqhgd,bkhd->bhgqk', qg, kb).astype(jnp.float32), NEG_INF)
        s_ctx = jnp.einsum('bqhgd,bkhd->bhgqk', qg, k_ctx)
        p_ctx, p_loc = sink_softmax(sink, (s_ctx, s_loc))
        o = (jnp.einsum('bhgqk,bkhd->bqhgd', p_ctx.astype(v.dtype), v_ctx)
             + jnp.einsum('bhgqk,bkhd->bqhgd', p_loc.astype(v.dtype), vb))
        return o.reshape(b, Q_BLOCK, SWA_HEADS * SWA_HD)

    return sweep_query_blocks(block, q)


def swa_context(q, k, v, sink):
    b, n = q.shape[:2]
    qg = q.reshape(b, n, SWA_KV_HEADS, SWA_GROUP, SWA_HD) * (SWA_HD ** -0.5)
    (p,) = sink_softmax(sink, (jnp.einsum('bqhgd,bkhd->bhgqk', qg, k),))
    o = jnp.einsum('bhgqk,bkhd->bqhgd', p.astype(v.dtype), v)
    return o.reshape(b, n, SWA_HEADS * SWA_HD)


def merge_branches(ys, z, gm, w_branch, w_out):
    terms = []
    for r, y in enumerate(ys):
        zr = z[..., r * BRANCH_W:(r + 1) * BRANCH_W]
        gr = gm[..., r * D_MODEL:(r + 1) * D_MODEL]
        terms.append(jax.nn.sigmoid(gr) * ((y * jax.nn.silu(zr)) @ w_branch[r]))
    return (terms[0] + terms[1] + terms[2]) @ w_out


def hybrid_layer(x, ctx, mod_lat, mod_ctx, rope_mla, rope_hd, norm_g, w_in,
                 mla_q_norm, mla_w_uq, mla_kv_norm, mla_w_ukv,
                 lam, lam_init, diff_subln, swa_sink, w_branch, w_out, need_ctx):
    shift, scale, gate = jnp.split(mod_lat[:, None, :], 3, axis=-1)
    shift_c, scale_c, gate_c = jnp.split(mod_ctx, 3, axis=-1)
    h_lat = rms_norm(x, norm_g) * (1.0 + scale) + shift
    h_ctx = rms_norm(ctx, norm_g) * (1.0 + scale_c) + shift_c
    qc_l, kvc_l, kr_l, dq_l, dk_l, dv_l, sq_l, sk_l, sv_l, z_l, gm_l = split_cols(h_lat @ w_in)
    qc_c, kvc_c, kr_c, dq_c, dk_c, dv_c, sq_c, sk_c, sv_c, z_c, gm_c = split_cols(h_ctx @ w_in)

    mq_l, mk_l, mv_l = mla_heads(qc_l, kvc_l, kr_l, mla_q_norm, mla_w_uq, mla_kv_norm, mla_w_ukv, rope_mla)
    mq_c, mk_c, mv_c = mla_heads(qc_c, kvc_c, kr_c, mla_q_norm, mla_w_uq, mla_kv_norm, mla_w_ukv, None)
    mk_all = jnp.concatenate([mk_c, mk_l], axis=1)
    mv_all = jnp.concatenate([mv_c, mv_l], axis=1)
    b, n = x.shape[:2]
    ya_l = sweep_query_blocks(lambda i, qb: dense_attend(qb, mk_all, mv_all), mq_l).reshape(b, n, BRANCH_W)

    dq_l, dk_l, dv_l = diff_heads(dq_l, dk_l, dv_l, rope_hd)
    dq_c, dk_c, dv_c = diff_heads(dq_c, dk_c, dv_c, None)
    dk_all = jnp.concatenate([dk_c, dk_l], axis=1)
    dv_all = jnp.concatenate([dv_c, dv_l], axis=1)
    yb_l = diff_post(sweep_query_blocks(lambda i, qb: diff_attend(qb, dk_all, dv_all, lam), dq_l),
                     diff_subln, lam_init)

    sq_l, sk_l, sv_l = swa_heads(sq_l, sk_l, sv_l, rope_hd)
    sq_c, sk_c, sv_c = swa_heads(sq_c, sk_c, sv_c, None)
    yc_l = swa_latent(sq_l, sk_l, sv_l, sk_c, sv_c, swa_sink)

    x = x + gate * merge_branches((ya_l, yb_l, yc_l), z_l, gm_l, w_branch, w_out)

    if need_ctx:
        bc, nc = ctx.shape[:2]
        ya_c = dense_attend(mq_c, mk_c, mv_c).reshape(bc, nc, BRANCH_W)
        yb_c = diff_post(diff_attend(dq_c, dk_c, dv_c, lam), diff_subln, lam_init)
        yc_c = swa_context(sq_c, sk_c, sv_c, swa_sink)
        ctx = ctx + gate_c * merge_branches((ya_c, yb_c, yc_c), z_c, gm_c, w_branch, w_out)
    return x, ctx


def setup_inputs(seed: int = 0) -> dict:
    key = jax.random.key(seed)
    ks = jax.random.split(key, 22)
    f32 = jnp.float32

    def nrm(k, shape, s):
        return jax.random.normal(k, shape, f32) * s

    L = DEPTH
    return {
        'x': nrm(ks[0], (BATCH, SEQ, D_MODEL), 1.0),
        'c': nrm(ks[1], (BATCH, D_MODEL), 1.0),
        'ctx': nrm(ks[2], (BATCH, CTX_LEN, D_MODEL), 1.0),
        'c_ctx': nrm(ks[3], (D_MODEL,), 1.0),
        'w_mod': nrm(ks[4], (L, D_MODEL, 3 * D_MODEL), D_MODEL ** -0.5),
        'b_mod': nrm(ks[5], (L, 3 * D_MODEL), 0.01),
        'norm_g': 1.0 + nrm(ks[6], (L, D_MODEL), 0.02),
        'w_in': nrm(ks[7], (L, D_MODEL, D_IN), D_MODEL ** -0.5),
        'mla_q_norm': 1.0 + nrm(ks[8], (L, MLA_Q_RANK), 0.02),
        'mla_w_uq': nrm(ks[9], (L, MLA_Q_RANK, MLA_HEADS * (MLA_NOPE + MLA_ROPE)), MLA_Q_RANK ** -0.5),
        'mla_kv_norm': 1.0 + nrm(ks[10], (L, MLA_KV_RANK), 0.02),
        'mla_w_ukv': nrm(ks[11], (L, MLA_KV_RANK, MLA_HEADS * (MLA_NOPE + MLA_V)), MLA_KV_RANK ** -0.5),
        'diff_lq1': nrm(ks[12], (L, DIFF_HD), 0.1),
        'diff_lk1': nrm(ks[13], (L, DIFF_HD), 0.1),
        'diff_lq2': nrm(ks[14], (L, DIFF_HD), 0.1),
        'diff_lk2': nrm(ks[15], (L, DIFF_HD), 0.1),
        'diff_subln': 1.0 + nrm(ks[16], (L, DIFF_V), 0.02),
        'swa_sink': nrm(ks[17], (L, SWA_HEADS), 0.5),
        'w_branch': nrm(ks[18], (L, N_BRANCH, BRANCH_W, D_MODEL), BRANCH_W ** -0.5),
        'w_out': nrm(ks[19], (L, D_MODEL, D_MODEL), D_MODEL ** -0.5),
        'final_norm': 1.0 + nrm(ks[20], (D_MODEL,), 0.02),
    }


def reference(x, c, ctx, c_ctx, w_mod, b_mod, norm_g, w_in, mla_q_norm, mla_w_uq, mla_kv_norm, mla_w_ukv,
              diff_lq1, diff_lk1, diff_lq2, diff_lk2, diff_subln, swa_sink, w_branch, w_out, final_norm):
    n_lat = x.shape[1]
    ROWS = n_lat // GRID_W
    row = jnp.repeat(jnp.arange(ROWS), GRID_W)
    col = jnp.tile(jnp.arange(GRID_W), ROWS)
    rope_mla = axial_rope_tables(row, col, MLA_ROPE)
    rope_hd = axial_rope_tables(row, col, SWA_HD)
    silu_c = jax.nn.silu(c)
    silu_cc = jax.nn.silu(c_ctx)
    f32 = jnp.float32
    for l in range(DEPTH):
        mod_lat = silu_c @ w_mod[l] + b_mod[l]
        mod_ctx = silu_cc @ w_mod[l] + b_mod[l]
        lam_init = 0.8 - 0.6 * math.exp(-0.3 * l)
        lam = (jnp.exp(jnp.sum(diff_lq1[l].astype(f32) * diff_lk1[l].astype(f32)))
               - jnp.exp(jnp.sum(diff_lq2[l].astype(f32) * diff_lk2[l].astype(f32))) + lam_init)
        x, ctx = hybrid_layer(x, ctx, mod_lat, mod_ctx, rope_mla, rope_hd, norm_g[l], w_in[l],
                              mla_q_norm[l], mla_w_uq[l], mla_kv_norm[l], mla_w_ukv[l],
                              lam, lam_init, diff_subln[l], swa_sink[l], w_branch[l], w_out[l],
                              l < DEPTH - 1)
    return rms_norm(x, final_norm)
```

```python
import math
import os
import numpy as np
import concourse.bass as bass
import concourse.mybir as mybir
from concourse.bass_utils import run_bass_kernel_spmd

F32 = mybir.dt.float32
BF16 = mybir.dt.bfloat16
AF = mybir.ActivationFunctionType
ALU = mybir.AluOpType
AX = mybir.AxisListType

D = 1024
SEQ = 8192
CTX = 256
T = SEQ + CTX
NT = T // 128
DEPTH = 2
DIN = 11424
NA = 5280
EPS = 1e-6
NCORES = 8
ARENA_F = 51200
DBG_NT = None
DBG_SKIP_W = False
DBG_SMALL = False
DBG_CUT = 99
SUM_MODE = 'hybrid'
DBG_D1 = False
DBG_BUNITS = None
DBG_SWA = True


class Sem:
    def __init__(self, h):
        self.h = h
        self.n = 0


class V:
    def __init__(self, ap, bufs):
        self.ap = ap
        self.bufs = bufs

    def __getitem__(self, k):
        return V(self.ap[k], self.bufs)

    def f(self, fn):
        return V(fn(self.ap), self.bufs)


class Buf:
    dram = False

    def __init__(self, ap, name=""):
        self.ap = ap
        self.w = None
        self.rs = {}
        self.dsem = None
        self.dsem_sw = None
        self.name = name

    def __getitem__(self, k):
        return V(self.ap[k], [self])

    def v(self, fn=None):
        return V(fn(self.ap) if fn else self.ap, [self])


class DBuf(Buf):
    dram = True

    def __init__(self, ap, name=""):
        Buf.__init__(self, ap, name)
        self.ws = {}


IN_KEYS = ('in_', 'in0', 'in1', 'lhsT', 'rhs', 'scalar', 'scalar1', 'scalar2', 'bias', 'scale', 'identity')
OUT_KEYS = ('out', 'accum_out', 'ap')
ENG = ('pe', 'act', 'dve', 'pool', 'sp')


class Prog:
    def __init__(self, nc, sems):
        self.nc = nc
        self.clk = {e: sems[i] for i, e in enumerate(ENG)}
        self.all_dsems = list(sems[len(ENG):])
        self.sw_sems = self.all_dsems[:10]
        self.hw_sems = self.all_dsems[10:]
        self.free_sw = list(self.sw_sems)
        self.free_hw = list(self.hw_sems)
        self.ops = {e: [] for e in ENG}
        self.waited = {e: {} for e in ENG}
        self.nops = 0

    def _waits_for(self, eng, reads, writes):
        w = {}

        def need(sem, val):
            if eng == 'pe' and sem is self.clk['pe']:
                return
            if w.get(sem, 0) < val:
                w[sem] = val

        for b in reads:
            if b.dram:
                for sem, val in b.ws.items():
                    need(sem, val)
            elif b.w:
                need(*b.w)
        for b in writes:
            if b.dram:
                for sem, val in b.rs.items():
                    need(sem, val)
            else:
                if b.w:
                    need(*b.w)
                for sem, val in b.rs.items():
                    need(sem, val)
        wd = self.waited[eng]
        out = []
        for sem, val in w.items():
            if wd.get(sem, 0) < val:
                wd[sem] = val
                out.append((sem, val))
        return out

    def _commit(self, tk, reads, writes):
        sem, val = tk
        for b in reads:
            if b.rs.get(sem, 0) < val:
                b.rs[sem] = val
        for b in writes:
            if b.dram:
                b.ws[sem] = val
            else:
                b.w = tk
                b.rs = {}

    def op(self, eng, name, **kw):
        reads, writes, kw2 = [], [], {}
        for k, v in kw.items():
            if isinstance(v, V):
                (writes if k in OUT_KEYS else reads).extend(v.bufs)
                kw2[k] = v.ap
            else:
                kw2[k] = v
        waits = self._waits_for(eng, reads, writes)
        clk = self.clk[eng]
        clk.n += 1
        tk = (clk, clk.n)
        self.ops[eng].append((waits, name, kw2, clk, 1))
        self._commit(tk, reads, writes)
        self.nops += 1
        return tk

    def dma(self, eng, pairs, sembuf, **extra):
        reads, writes = [], []
        for o, i in pairs:
            reads.extend(i.bufs)
            writes.extend(o.bufs)
        if eng == 'pool':
            if sembuf.dsem_sw is None:
                sembuf.dsem_sw = self.free_sw.pop()
            sem = sembuf.dsem_sw
        else:
            if sembuf.dsem is None:
                sembuf.dsem = self.free_hw.pop()
            sem = sembuf.dsem
        waits = self._waits_for(eng, reads, writes)
        for idx, (o, i) in enumerate(pairs):
            sem.n += 16
            kw = dict(out=o.ap, in_=i.ap)
            kw.update(extra)
            self.ops[eng].append((waits if idx == 0 else [], 'dma_start', kw, sem, 16))
            self.nops += 1
        tk = (sem, sem.n)
        self._commit(tk, reads, writes)
        return tk

    def barrier(self):
        for e in ENG:
            waits = []
            wd = self.waited[e]
            for sem in list(self.clk.values()) + self.all_dsems:
                if sem is self.clk[e]:
                    continue
                if sem.n > 0 and wd.get(sem, 0) < sem.n:
                    wd[sem] = sem.n
                    waits.append((sem, sem.n))
            if waits:
                self.ops[e].append((waits, None, None, None, 0))
        self.free_sw = list(self.sw_sems)
        self.free_hw = list(self.hw_sems)

    def emit(self):
        nc = self.nc
        with nc.Block() as block:
            def mk(e):
                def f(eng):
                    for waits, name, kw, sem, amt in self.ops[e]:
                        for s_, val in waits:
                            eng.wait_ge(s_.h, val)
                        if name is None:
                            continue
                        ins = getattr(eng, name)(**kw)
                        ins.then_inc(sem.h, amt)
                return f
            block.tensor(mk('pe'))
            block.scalar(mk('act'))
            block.vector(mk('dve'))
            block.gpsimd(mk('pool'))
            block.sync(mk('sp'))


class Arena:
    def __init__(self, t, n):
        self.t = t
        self.n = n
        self.off = 0

    def reset(self, off=0):
        self.off = off

    def alloc(self, shape, dt, name=""):
        parts = shape[0]
        nel = 1
        for s_ in shape[1:]:
            nel *= s_
        nb = nel * (2 if dt == BF16 else 4)
        nfl = ((nb + 3) // 4 + 7) // 8 * 8
        assert self.off + nfl <= self.n, f"arena overflow at {name}: {self.off}+{nfl}>{self.n}"
        ap = self.t[0:parts, self.off:self.off + nfl]
        if dt == BF16:
            ap = ap.bitcast(BF16)
        ap = ap[:, 0:nel]
        if len(shape) == 3:
            ap = ap.rearrange("p (a b) -> p a b", a=shape[1])
        elif len(shape) == 4:
            ap = ap.rearrange("p (a b c) -> p a b c", a=shape[1], b=shape[2])
        self.off += nfl
        return Buf(ap, name)


def bc(ap, axis, shape):
    return ap.unsqueeze(axis).to_broadcast(list(shape))


def build_program(debug=None, stop_after=None, layers=DEPTH):
    nc = bass.Bass("TRN2", target_bir_lowering=False)

    def din(name, shape):
        return DBuf(nc.dram_tensor(name, list(shape), F32, kind="ExternalInput").ap(), name)

    x_in = din("x_b", [SEQ if not DBG_SMALL else (DBG_NT - 2) * 128, D])
    ctx_in = din("ctx_b", [CTX, D])
    c2_in = din("c2", [128, 8, 2])
    bmodT_in = din("bmodT", [DEPTH, 128, 24])
    ngT_in = din("ngT", [DEPTH, 128, 8])
    qnT_in = din("qnT", [DEPTH, 128, 3])
    kvnT_in = din("kvnT", [DEPTH, 128, 2])
    lqk_in = din("lqk", [DEPTH, 1, 256])
    sublnT_in = din("sublnT", [DEPTH, 128, 1])
    sink_in = din("sink", [DEPTH, 1, 16])
    fing_in = din("fing", [1, D])
    w_mod_in = din("w_mod", [DEPTH if not DBG_SMALL else 1, D, 3 * D])
    w_in_in = din("w_in", ([DEPTH if not DBG_D1 else 1, D, DIN]) if not DBG_SMALL else [1, D, NA])
    w_uq_in = din("mla_w_uq", [DEPTH, 384, 768])
    w_ukv_in = din("mla_w_ukv", [DEPTH, 256, 1536])
    w_br_in = din("w_branch", ([DEPTH if not DBG_D1 else 1, 3, D, D]) if not DBG_SMALL else [1, 1, 128, 128])
    w_out_in = din("w_out", [DEPTH, D, D] if not DBG_SMALL else [1, 128, 128])
    ident_in = din("ident", [128, 128])
    maskL_in = din("maskL", [128, 128])
    maskR_in = din("maskR", [128, 128])
    rope_in = din("rope", [T if not DBG_SMALL else DBG_NT * 128, 192])

    dbg = set(debug or [])

    def dscr(name, shape, dt):
        kind = "ExternalOutput" if name in dbg else "Internal"
        return DBuf(nc.dram_tensor(name, list(shape), dt, kind=kind).ap(), name)

    out_d = DBuf(nc.dram_tensor("out", [SEQ, D], F32, kind="ExternalOutput").ap(), "out")
    X1 = dscr("X1", [T, D], F32)
    GATE = dscr("GATE", [DEPTH, 2, D], F32)
    WM = dscr("WM", [DEPTH, 10, 128, 8, 1024], BF16)
    HT = dscr("HT", [NT, 128, 8, 128], BF16)
    QTm = dscr("QTm", [8, 96, T], BF16)
    KTm = dscr("KTm", [4, 128, T], BF16)
    KRm = dscr("KRm", [32, T], BF16)
    Vm = dscr("Vm", [T, 1024], BF16)
    QTd = dscr("QTd", [8, 128, T], BF16)
    KTd = dscr("KTd", [8, 128, T], BF16)
    Vd = dscr("Vd", [T, 1024], BF16)
    QTs = dscr("QTs", [8, 128, T], BF16)
    KTs = dscr("KTs", [2, 128, T], BF16)
    Vs = dscr("Vs", [T, 256], BF16)
    YT = dscr("YT", [3, 128, 8, T], BF16)

    import contextlib
    es = contextlib.ExitStack()
    with es:
        arena_t = es.enter_context(nc.sbuf_tensor("arena", [128, ARENA_F], F32))
        ps_t = es.enter_context(nc.psum_tensor("ps", [128, 8, 512], F32))
        sems = [Sem(es.enter_context(nc.semaphore(f"s{i}"))) for i in range(96)]
        P = Prog(nc, sems)
        A = Arena(arena_t, ARENA_F)
        PB = [Buf(ps_t[:, b, :], f"bank{b}") for b in range(8)]

        def pv(b0, nb=1):
            if nb == 1:
                return V(ps_t[:, b0, :], [PB[b0]])
            return V(ps_t[:, b0:b0 + nb, :], PB[b0:b0 + nb])

        def pv_bf(b0, nb=1):
            ap = ps_t[:, b0:b0 + nb, :].rearrange("p a b -> p (a b)").bitcast(BF16)
            return V(ap, PB[b0:b0 + nb])

        ident_b = A.alloc([128, 128], BF16, "ident_b")
        ones_b = A.alloc([128, 128], BF16, "ones_b")
        ones_f = A.alloc([128, 128], F32, "ones_f")
        maskL = A.alloc([128, 128], BF16, "maskL")
        maskR = A.alloc([128, 128], BF16, "maskR")
        fing = A.alloc([128, D], F32, "fing")
        modc = []
        for l in range(DEPTH):
            modc.append(dict(
                A_lat=A.alloc([128, 8], F32), S_lat=A.alloc([128, 8], F32),
                A_ctx=A.alloc([128, 8], F32), S_ctx=A.alloc([128, 8], F32),
                nlam=A.alloc([128, 1], F32), gcol=A.alloc([128, 1], F32),
                esink=A.alloc([128, 16], F32)))
        base_off = A.off

        P.dma('pool', [(ident_b.v(), ident_in.v())], ident_b)
        P.dma('pool', [(maskL.v(), maskL_in.v())], maskL)
        P.dma('pool', [(maskR.v(), maskR_in.v())], maskR)
        P.dma('sp', [(fing.v(), fing_in.v(lambda a: a.partition_broadcast(128)))], fing)
        P.op('dve', 'memset', ap=ones_f.v(), constant=1.0)
        P.op('dve', 'tensor_copy', out=ones_b.v(), in_=ones_f.v())

        A.reset(base_off)
        c2 = A.alloc([128, 8, 2], F32, "c2")
        sc = A.alloc([128, 8, 2], F32, "sc")
        wm = A.alloc([128, 8, 3 * D], F32, "wm")
        bmodT = A.alloc([128, 24], F32)
        modT = A.alloc([128, 24, 2], F32)
        ngT = A.alloc([128, 8], F32)
        tmp8 = A.alloc([128, 8], F32)
        lqk = A.alloc([128, 4, 64], F32)
        lprod = A.alloc([128, 2, 64], F32)
        lsum = A.alloc([128, 2], F32)
        lexp = A.alloc([128, 2], F32)
        subl = A.alloc([128, 1], F32)
        sinkb = A.alloc([128, 16], F32)
        P.dma('sp', [(c2.v(), c2_in.v())], c2)
        P.op('act', 'activation', out=sc.v(), in_=c2.v(), func=AF.Silu)
        for l in range(layers):
            mc = modc[l]
            lam_init = 0.8 - 0.6 * math.exp(-0.3 * l)
            P.dma('sp', [(wm[:, k, :], w_mod_in[l, k * 128:(k + 1) * 128, :]) for k in range(8)], wm)
            P.dma('sp', [(bmodT.v(), bmodT_in[l])], bmodT)
            P.dma('sp', [(ngT.v(), ngT_in[l])], ngT)
            for j in range(24):
                for k in range(8):
                    P.op('pe', 'matmul', out=pv(0)[:, 2 * j:2 * j + 2], lhsT=wm[:, k, j * 128:(j + 1) * 128],
                         rhs=sc[:, k, :], start=(k == 0), stop=(k == 7))
            P.op('dve', 'tensor_tensor', out=modT.v(), in0=pv(0)[:, 0:48].f(lambda a: a.rearrange("p (j t) -> p j t", t=2)),
                 in1=bmodT.v(lambda a: bc(a, 2, [128, 24, 2])), op=ALU.add)
            for kind, (ka, ks) in enumerate((("A_lat", "S_lat"), ("A_ctx", "S_ctx"))):
                P.op('dve', 'tensor_scalar', out=tmp8.v(), in0=modT[:, 8:16, kind], scalar1=1.0, scalar2=None, op0=ALU.add)
                P.op('dve', 'tensor_tensor', out=mc[ka].v(), in0=tmp8.v(), in1=ngT.v(), op=ALU.mult)
                P.op('dve', 'tensor_copy', out=mc[ks].v(), in_=modT[:, 0:8, kind])
                P.dma('sp', [(GATE[l, kind].f(lambda a: a.rearrange("(c p) -> p c", p=128)), modT[:, 16:24, kind])],
                      modT, allow_slow_non_contiguous=True)
            P.dma('sp', [(lqk.v(lambda a: a.rearrange("p a b -> p (a b)")), lqk_in[l].f(lambda a: a.partition_broadcast(128)))], lqk)
            P.op('dve', 'tensor_tensor', out=lprod[:, 0, :], in0=lqk[:, 0, :], in1=lqk[:, 1, :], op=ALU.mult)
            P.op('dve', 'tensor_tensor', out=lprod[:, 1, :], in0=lqk[:, 2, :], in1=lqk[:, 3, :], op=ALU.mult)
            P.op('dve', 'reduce_sum', out=lsum.v(), in_=lprod.v(), axis=AX.X)
            P.op('act', 'activation', out=lexp.v(), in_=lsum.v(), func=AF.Exp)
            P.op('dve', 'tensor_tensor', out=mc['nlam'].v(), in0=lexp[:, 1:2], in1=lexp[:, 0:1], op=ALU.subtract)
            P.op('dve', 'tensor_scalar', out=mc['nlam'].v(), in0=mc['nlam'].v(), scalar1=-lam_init, scalar2=None, op0=ALU.add)
            P.dma('sp', [(subl.v(), sublnT_in[l])], subl)
            P.op('dve', 'tensor_scalar', out=mc['gcol'].v(), in0=subl.v(), scalar1=(1.0 - lam_init), scalar2=None, op0=ALU.mult)
            P.dma('sp', [(sinkb.v(), sink_in[l].f(lambda a: a.partition_broadcast(128)))], sinkb)
            P.op('act', 'activation', out=mc['esink'].v(), in_=sinkb.v(), func=AF.Exp)
        P.barrier()

        A.reset(base_off)
        wslot = [A.alloc([128, 8, 1024], BF16, f"wslot{i}") for i in range(3)]
        cnt = 0
        for l in range(0 if DBG_SKIP_W else layers):
            for u in range(10):
                if u < 9:
                    r, kind = divmod(u, 3)
                    if kind == 0:
                        src = w_in_in[l, :, NA + r * 1024:NA + (r + 1) * 1024]
                    elif kind == 1:
                        src = w_in_in[l, :, NA + 3072 + r * 1024:NA + 3072 + (r + 1) * 1024]
                    else:
                        src = w_br_in[l, r]
                else:
                    src = w_out_in[l]
                ws = wslot[cnt % 3]
                cnt += 1
                P.dma('pool', [(ws[:, k, :], src[k * 128:(k + 1) * 128, :]) for k in range(8)], ws)
                P.dma('sp', [(WM[l, u], ws.v())], ws)
        P.barrier()
        if stop_after == 'W':
            P.emit()
            return nc

        for l in range(layers):
            mc = modc[l]
            phase_A(nc, P, A, base_off, l, mc, locals())
            P.barrier()
            if stop_after == ('A', l):
                break
            phase_B(nc, P, A, base_off, l, mc, locals())
            P.barrier()
            if stop_after == ('B', l):
                break
            phase_C(nc, P, A, base_off, l, mc, locals())
            P.barrier()
        P.emit()
    return nc


def phase_A(nc, P, A, base_off, l, mc, G):
    pv, pv_bf = G['pv'], G['pv_bf']
    ident_b = G['ident_b']
    A.reset(base_off)
    WA = A.alloc([128, 8, NA], BF16, "WA")
    WUQ = A.alloc([128, 3, 768], BF16, "WUQ")
    WUKV = A.alloc([128, 2, 1536], BF16, "WUKV")
    wq_f = A.alloc([128, 3, 768], F32, "wq_f")
    wkv_f = A.alloc([128, 2, 1536], F32, "wkv_f")
    qnT = A.alloc([128, 3], F32)
    kvnT = A.alloc([128, 2], F32)
    xt = [A.alloc([128, D], F32, f"xt{i}") for i in range(2)]
    rp = [A.alloc([128, 192], F32, f"rp{i}") for i in range(2)]
    junk = A.alloc([128, D], BF16, "junk")
    ssq = A.alloc([128, 4], F32, "ssq")
    rstd = A.alloc([128, 4], F32, "rstd")
    xn = A.alloc([128, D], BF16, "xn")
    hT = [A.alloc([128, 8, 128], BF16, f"hT{i}") for i in range(2)]
    t1 = A.alloc([128, 1024], F32, "t1")
    t2 = A.alloc([128, 1024], F32, "t2")
    t3 = A.alloc([128, 1024], F32, "t3")
    qcn = A.alloc([128, 640], BF16, "qcn")
    qcnT = A.alloc([128, 5, 128], BF16, "qcnT")
    rq = [A.alloc([128, 1024], BF16, f"rq{i}") for i in range(2)]
    stg = [A.alloc([128, 8, 128], BF16, f"stg{i}") for i in range(3)]
    vst = [A.alloc([128, 1024], BF16, f"vst{i}") for i in range(2)]
    qm = A.alloc([128, 8, 96], BF16, "qm")
    qf = A.alloc([128, 768], F32, "qf")
    rsrc = A.alloc([128, 1024], F32, "rsrc")
    knb = A.alloc([128, 512], BF16, "knb")
    krb = A.alloc([128, 32], BF16, "krb")
    krs = A.alloc([128, 128], BF16, "krs")

    w_in_in, w_uq_in, w_ukv_in = G['w_in_in'], G['w_uq_in'], G['w_ukv_in']
    for k in range(8):
        P.dma('pool', [(WA[:, k, :], w_in_in[l, k * 128:(k + 1) * 128, 0:NA])], WA)
    P.dma('sp', [(wq_f[:, k, :], w_uq_in[l, k * 128:(k + 1) * 128, :]) for k in range(3)], wq_f)
    P.dma('sp', [(wkv_f[:, k, 0:512].f(lambda a: a.rearrange("p (h d) -> p h d", h=8)),
                  w_ukv_in[l, k * 128:(k + 1) * 128, :].f(lambda a: a.rearrange("p (h d) -> p h d", h=8)[:, :, 0:64])) for k in range(2)]
          + [(wkv_f[:, k, 512:1536].f(lambda a: a.rearrange("p (h d) -> p h d", h=8)),
              w_ukv_in[l, k * 128:(k + 1) * 128, :].f(lambda a: a.rearrange("p (h d) -> p h d", h=8)[:, :, 64:192])) for k in range(2)],
          wkv_f)
    P.dma('sp', [(qnT.v(), G['qnT_in'][l])], qnT)
    P.dma('sp', [(kvnT.v(), G['kvnT_in'][l])], kvnT)
    P.op('dve', 'tensor_tensor', out=WUQ.v(), in0=wq_f.v(), in1=qnT.v(lambda a: bc(a, 2, [128, 3, 768])), op=ALU.mult)
    P.op('dve', 'tensor_tensor', out=WUKV.v(), in0=wkv_f.v(), in1=kvnT.v(lambda a: bc(a, 2, [128, 2, 1536])), op=ALU.mult)

    x_src = G['x_in'] if l == 0 else G['X1']
    ctx_src = G['ctx_in'] if l == 0 else G['X1']

    def load_tile(i):
        s = i % 2
        if i < 2:
            src = ctx_src[i * 128:(i + 1) * 128, :]
        else:
            src = x_src[(i - 2) * 128:(i - 1) * 128, :] if l == 0 else x_src[i * 128:(i + 1) * 128, :]
        P.dma('sp', [(xt[s].v(), src)], xt[s])
        P.dma('sp', [(rp[s].v(), G['rope_in'][i * 128:(i + 1) * 128, :])], rp[s])

    pstate = {'n': 0}

    def ppair():
        b = (pstate['n'] % 4) * 2
        pstate['n'] += 1
        return b

    def rms_from(src_v, n, col):
        P.op('act', 'activation', out=junk[:, 0:n], in_=src_v, func=AF.Square, accum_out=ssq[:, col:col + 1])
        P.op('act', 'activation', out=ssq[:, col:col + 1], in_=ssq[:, col:col + 1], func=AF.Sqrt, scale=1.0 / n, bias=EPS)
        P.op('dve', 'reciprocal', out=rstd[:, col:col + 1], in_=ssq[:, col:col + 1])

    def rope(src, H, hd, cs, sn, dst):
        hf = hd // 2
        n = H * hd
        for c0 in range(0, n, 512):
            c1 = min(n, c0 + 512)
            P.op('act', 'activation', out=rsrc[:, c0:c1], in_=src[:, c0:c1], func=AF.Copy)
        s3 = rsrc[:, 0:n].f(lambda a: a.rearrange("p (h d) -> p h d", h=H))
        a1 = t1[:, 0:n].f(lambda a: a.rearrange("p (h d) -> p h d", h=H))
        a2 = t2[:, 0:n].f(lambda a: a.rearrange("p (h d) -> p h d", h=H))
        P.op('dve', 'tensor_tensor', out=a1, in0=s3, in1=cs.f(lambda a: bc(a, 1, [128, H, hd])), op=ALU.mult)
        P.op('dve', 'tensor_tensor', out=a2[:, :, 0:hf], in0=s3[:, :, hf:hd],
             in1=sn[:, 0:hf].f(lambda a: bc(a, 1, [128, H, hf])), op=ALU.mult)
        P.op('dve', 'tensor_tensor', out=a2[:, :, hf:hd], in0=s3[:, :, 0:hf],
             in1=sn[:, hf:hd].f(lambda a: bc(a, 1, [128, H, hf])), op=ALU.mult)
        P.op('dve', 'tensor_tensor', out=dst, in0=t1[:, 0:n], in1=t2[:, 0:n], op=ALU.add)

    sidx = {'n': 0}

    def transpose_store(src_b, nblk, width, dsts, tok0):
        b = ppair()
        pt = pv_bf(b, 1)
        for j in range(nblk):
            P.op('pe', 'transpose', out=pt[0:width, j * 128:(j + 1) * 128], in_=src_b[:, j * width:(j + 1) * width],
                 identity=ident_b.v())
        st = stg[sidx['n'] % 3]
        sidx['n'] += 1
        P.op('act', 'activation', out=st[0:width, 0:nblk, :],
             in_=pt[0:width, 0:nblk * 128].f(lambda a: a.rearrange("p (j t) -> p j t", j=nblk)), func=AF.Copy)
        P.dma('sp', [(dsts(j), st[0:width, j, :]) for j in range(nblk)], st)

    groups = [(0, 384), (384, 288), (672, 512), (1184, 512), (1696, 512), (2208, 512), (2720, 512), (3232, 512),
              (3744, 512), (4256, 512), (4768, 512)]

    def proj(h, gi, out_v):
        c0, w = groups[gi]
        for k in range(8):
            P.op('pe', 'matmul', out=out_v[:, 0:w], lhsT=h[:, k, :], rhs=WA[:, k, c0:c0 + w], start=(k == 0), stop=(k == 7))

    NTL = DBG_NT or NT
    ssq0 = A.alloc([128, 1], F32, "ssq0")
    rstd0 = A.alloc([128, 1], F32, "rstd0")
    junk0 = A.alloc([128, D], BF16, "junk0")

    def head_a(i):
        s = i % 2
        P.op('act', 'activation', out=junk0.v(), in_=xt[s].v(), func=AF.Square, accum_out=ssq0.v())
        P.op('act', 'activation', out=ssq0.v(), in_=ssq0.v(), func=AF.Sqrt, scale=1.0 / D, bias=EPS)
        P.op('dve', 'reciprocal', out=rstd0.v(), in_=ssq0.v())
        P.op('act', 'activation', out=xn.v(), in_=xt[s].v(), func=AF.Copy, scale=rstd0[:, 0:1])

    def head_b(i):
        s = i % 2
        Acol, Scol = (mc['A_ctx'], mc['S_ctx']) if i < 2 else (mc['A_lat'], mc['S_lat'])
        b = ppair()
        pt = pv_bf(b, 1)
        for k in range(8):
            P.op('pe', 'transpose', out=pt[:, k * 128:(k + 1) * 128], in_=xn[:, k * 128:(k + 1) * 128], identity=ident_b.v())
        h = hT[s]
        pt3 = pt[:, 0:1024].f(lambda a: a.rearrange("p (k t) -> p k t", k=8))
        t13 = t3.v(lambda a: a.rearrange("p (k t) -> p k t", k=8))
        P.op('dve', 'tensor_tensor', out=t13, in0=pt3, in1=Acol.v(lambda a: bc(a, 2, [128, 8, 128])), op=ALU.mult)
        P.op('dve', 'tensor_tensor', out=h.v(), in0=t13, in1=Scol.v(lambda a: bc(a, 2, [128, 8, 128])), op=ALU.add)
        P.dma('sp', [(G['HT'][i], h.v())], h)

    def part_a(i):
        s = i % 2
        tok0 = i * 128
        h = hT[s]
        cs_m, sn_m = rp[s][:, 128:160], rp[s][:, 160:192]
        b01 = ppair()
        proj(h, 0, pv(b01))
        proj(h, 1, pv(b01 + 1))
        rms_from(pv(b01)[:, 0:384], 384, 1)
        rms_from(pv(b01 + 1)[:, 0:256], 256, 2)
        P.op('act', 'activation', out=qcn[:, 0:384], in_=pv(b01)[:, 0:384], func=AF.Copy, scale=rstd[:, 1:2])
        P.op('act', 'activation', out=qcn[:, 384:640], in_=pv(b01 + 1)[:, 0:256], func=AF.Copy, scale=rstd[:, 2:3])
        rope(pv(b01 + 1)[:, 256:288], 1, 32, cs_m, sn_m, krb.v())
        bt = ppair()
        ptq = pv_bf(bt, 1)
        for j in range(5):
            P.op('pe', 'transpose', out=ptq[:, j * 128:(j + 1) * 128], in_=qcn[:, j * 128:(j + 1) * 128], identity=ident_b.v())
        P.op('act', 'activation', out=qcnT.v(), in_=ptq[:, 0:640].f(lambda a: a.rearrange("p (j t) -> p j t", j=5)), func=AF.Copy)
        bq = ppair()
        for g2 in range(2):
            for k in range(3):
                P.op('pe', 'matmul', out=pv(bq + g2)[:, 0:384], lhsT=qcnT[:, k, :], rhs=WUQ[:, k, g2 * 384:(g2 + 1) * 384],
                     start=(k == 0), stop=(k == 2))
        bk = ppair()
        for k in range(2):
            P.op('pe', 'matmul', out=pv(bk), lhsT=qcnT[:, 3 + k, :], rhs=WUKV[:, k, 0:512], start=(k == 0), stop=(k == 1))
        bv = ppair()
        for g2 in range(2):
            for k in range(2):
                P.op('pe', 'matmul', out=pv(bv + g2), lhsT=qcnT[:, 3 + k, :], rhs=WUKV[:, k, 512 + g2 * 512:1024 + g2 * 512],
                     start=(k == 0), stop=(k == 1))
        for g2 in range(2):
            P.op('act', 'activation', out=qf[:, g2 * 384:(g2 + 1) * 384], in_=pv(bq + g2)[:, 0:384], func=AF.Copy)
        q3 = qf.v(lambda a: a.rearrange("p (h d) -> p h d", h=8))
        P.op('act', 'activation', out=qm[:, :, 0:64], in_=q3[:, :, 0:64], func=AF.Copy)
        a1 = t1[:, 0:256].f(lambda a: a.rearrange("p (h d) -> p h d", h=8))
        a2 = t2[:, 0:256].f(lambda a: a.rearrange("p (h d) -> p h d", h=8))
        P.op('dve', 'tensor_tensor', out=a1, in0=q3[:, :, 64:96], in1=cs_m.f(lambda a: bc(a, 1, [128, 8, 32])), op=ALU.mult)
        P.op('dve', 'tensor_tensor', out=a2[:, :, 0:16], in0=q3[:, :, 80:96],
             in1=sn_m[:, 0:16].f(lambda a: bc(a, 1, [128, 8, 16])), op=ALU.mult)
        P.op('dve', 'tensor_tensor', out=a2[:, :, 16:32], in0=q3[:, :, 64:80],
             in1=sn_m[:, 16:32].f(lambda a: bc(a, 1, [128, 8, 16])), op=ALU.mult)
        P.op('dve', 'tensor_tensor', out=qm[:, :, 64:96], in0=a1, in1=a2, op=ALU.add)
        P.op('act', 'activation', out=knb.v(), in_=pv(bk), func=AF.Copy)
        vs_ = vst[i % 2]
        P.op('act', 'activation', out=vs_.v(), in_=pv(bv, 2).f(lambda a: a.rearrange("p a b -> p (a b)")), func=AF.Copy)
        P.dma('sp', [(G['Vm'][tok0:tok0 + 128, :], vs_.v())], vs_)
        transpose_store(qm.v(lambda a: a.rearrange("p h d -> p (h d)")), 8, 96,
                        lambda j: G['QTm'][j, :, tok0:tok0 + 128], tok0)
        transpose_store(knb.v(), 4, 128, lambda j: G['KTm'][j, :, tok0:tok0 + 128], tok0)
        bkr = ppair()
        ptk = pv_bf(bkr, 1)
        P.op('pe', 'transpose', out=ptk[0:32, 0:128], in_=krb.v(), identity=ident_b.v())
        P.op('act', 'activation', out=krs[0:32, :], in_=ptk[0:32, 0:128], func=AF.Copy)
        P.dma('sp', [(G['KRm'][:, tok0:tok0 + 128], krs[0:32, :])], krs)

    def part_b1(i):
        s = i % 2
        tok0 = i * 128
        h = hT[s]
        cs_hd, sn_hd = rp[s][:, 0:64], rp[s][:, 64:128]
        for gi, dstT in ((2, G['QTd']), (4, G['KTd'])):
            bp = ppair()
            proj(h, gi, pv(bp))
            proj(h, gi + 1, pv(bp + 1))
            r_ = rq[gi // 2 % 2]
            rope(pv(bp, 2).f(lambda a: a.rearrange("p a b -> p (a b)")), 16, 64, cs_hd, sn_hd, r_.v())
            transpose_store(r_.v(), 8, 128, (lambda dT: (lambda j: dT[j, :, tok0:tok0 + 128]))(dstT), tok0)

    def part_b2(i):
        s = i % 2
        tok0 = i * 128
        h = hT[s]
        cs_hd, sn_hd = rp[s][:, 0:64], rp[s][:, 64:128]
        bp = ppair()
        proj(h, 6, pv(bp))
        proj(h, 7, pv(bp + 1))
        vs_ = vst[(i + 1) % 2]
        P.op('act', 'activation', out=vs_.v(), in_=pv(bp, 2).f(lambda a: a.rearrange("p a b -> p (a b)")), func=AF.Copy)
        P.dma('sp', [(G['Vd'][tok0:tok0 + 128, :], vs_.v())], vs_)
        bp = ppair()
        proj(h, 8, pv(bp))
        proj(h, 9, pv(bp + 1))
        r_ = rq[0]
        rope(pv(bp, 2).f(lambda a: a.rearrange("p a b -> p (a b)")), 16, 64, cs_hd, sn_hd, r_.v())
        transpose_store(r_.v(), 8, 128, lambda j: G['QTs'][j, :, tok0:tok0 + 128], tok0)
        bp = ppair()
        proj(h, 10, pv(bp))
        r_ = rq[1]
        rope(pv(bp)[:, 0:256], 4, 64, cs_hd, sn_hd, r_[:, 0:256])
        P.op('act', 'activation', out=r_[:, 256:512], in_=pv(bp)[:, 256:512], func=AF.Copy)
        P.dma('sp', [(G['Vs'][tok0:tok0 + 128, :], r_[:, 256:512])], r_)
        transpose_store(r_[:, 0:256], 2, 128, lambda j: G['KTs'][j, :, tok0:tok0 + 128], tok0)

    load_tile(0)
    head_a(0)
    head_b(0)
    for i in range(NTL):
        if i + 1 < NTL:
            load_tile(i + 1)
        part_a(i)
        if i + 1 < NTL:
            head_a(i + 1)
        part_b1(i)
        if i + 1 < NTL:
            head_b(i + 1)
        part_b2(i)


def phase_B(nc, P, A, base_off, l, mc, G):
    pv = G['pv']
    ps_t = G['ps_t']
    PB = G['PB']
    ones_b, ones_f, ident_b = G['ones_b'], G['ones_f'], G['ident_b']
    NTL = DBG_NT or NT
    TL = NTL * 128
    NLQ = (NTL - 2) // 4
    A.reset(base_off)
    KT = [A.alloc([128, T], BF16, f"KT{i}") for i in range(2)]
    QT = [A.alloc([128, T], BF16, f"QT{i}") for i in range(2)]
    VV = [A.alloc([128, NT, 128], BF16, f"VV{i}") for i in range(2)]
    PT = [A.alloc([128, 2, 512], BF16, f"PT{i}") for i in range(3)]
    rs = A.alloc([128, 512], F32, "rs")
    On = [A.alloc([128, 512], F32, f"On{i}") for i in range(2)]
    Oc = A.alloc([128, 512], F32, "Oc")
    sqb = A.alloc([128, 512], F32, "sqb")
    rstd_b = A.alloc([128, 512], F32, "rstd_b")
    yst = [A.alloc([128, 512], BF16, f"yst{i}") for i in range(2)]
    QTm, KTm, KRm, Vm, QTd, KTd, Vd, YT = (G[k] for k in ('QTm', 'KTm', 'KRm', 'Vm', 'QTd', 'KTd', 'Vd', 'YT'))

    units = [('mla', h) for h in range(8)] + [('diff', h) for h in range(8)]
    if DBG_BUNITS is not None:
        units = list(DBG_BUNITS)

    def load_unit(u):
        kind, h = units[u]
        s = u % 2
        if kind == 'mla':
            P.dma('sp', [(KT[s][0:64, 0:TL], KTm[h // 2, (h % 2) * 64:(h % 2) * 64 + 64, 0:TL]),
                         (KT[s][64:96, 0:TL], KRm[:, 0:TL])], KT[s])
            P.dma('sp', [(QT[s][0:96, 0:TL], QTm[h, :, 0:TL])], QT[s])
            vsrc = Vm
        else:
            P.dma('sp', [(KT[s][:, 0:TL], KTd[h, :, 0:TL])], KT[s])
            P.dma('sp', [(QT[s][:, 0:TL], QTd[h, :, 0:TL])], QT[s])
            vsrc = Vd
        P.dma('sp', [(VV[s][:, 0:NTL, :], vsrc[0:TL, h * 128:(h + 1) * 128].f(lambda a: a.rearrange("(c p) d -> p c d", p=128)))], VV[s])

    st = {'g': 0, 'acc': 0, 'y': 0}
    ACC = [[A.alloc([128, 2, 512], F32, f"ACC{i}{j}") for j in range(2)] for i in range(2)]
    tsum = [A.alloc([128, 512], F32, f"tsum{i}") for i in range(2)]
    tsb = [A.alloc([128, 512], BF16, f"tsb{i}") for i in range(2)]
    deferred = []

    def pop_stage():
        if deferred:
            _, fn = deferred.pop(0)
            if fn is not None:
                fn()

    def flush_upto(pidx):
        while deferred and deferred[0][0] <= pidx:
            _, fn = deferred.pop(0)
            if fn is not None:
                fn()

    def flush_all():
        flush_upto(1 << 60)

    def defer(fns):
        pidx = st['acc'] - 1
        deferred.extend((pidx, fn) for fn in fns)

    def attn_pass(s, q0, qn, kcs, r0, r1, scale):
        assert len(kcs) % 2 == 0
        flush_upto(st['acc'] - 2)
        acc = st['acc'] % 2
        st['acc'] += 1
        bO, bS = 4 + 2 * acc, 5 + 2 * acc
        groups = [kcs[i:i + 2] for i in range(0, len(kcs), 2)]

        def qk(gi):
            gg = st['g'] + gi
            Sb = (gg % 2) * 2
            for j, kc in enumerate(groups[gi]):
                P.op('pe', 'matmul', out=pv(Sb + j)[:, 0:qn], lhsT=KT[s][r0:r1, kc * 128:(kc + 1) * 128],
                     rhs=QT[s][r0:r1, q0:q0 + qn], start=True, stop=True)

        qk(0)
        for gi, grp in enumerate(groups):
            gg = st['g'] + gi
            Sb = (gg % 2) * 2
            pt = PT[gg % 3]
            if gi + 1 < len(groups):
                qk(gi + 1)
            P.op('act', 'activation', out=pt[:, :, 0:qn], in_=V(ps_t[:, Sb:Sb + 2, 0:qn], PB[Sb:Sb + 2]), func=AF.Exp, scale=scale)
            use_pe = (SUM_MODE == 'pe') or (SUM_MODE == 'hybrid' and gi % 2 == 1)
            if not use_pe:
                ac = ACC[acc][gi % 2] if SUM_MODE == 'dve' else ACC[acc][0]
                if gi < 2:
                    P.op('dve', 'tensor_copy', out=ac[:, :, 0:qn], in_=pt[:, :, 0:qn])
                else:
                    P.op('dve', 'tensor_tensor', out=ac[:, :, 0:qn], in0=ac[:, :, 0:qn], in1=pt[:, :, 0:qn], op=ALU.add)
            for j, kc in enumerate(grp):
                first = (gi == 0 and j == 0)
                last = (gi == len(groups) - 1 and j == 1)
                P.op('pe', 'matmul', out=pv(bO)[:, 0:qn], lhsT=VV[s][:, kc, :], rhs=pt[:, j, 0:qn], start=first, stop=last)
                if use_pe:
                    if SUM_MODE == 'pe':
                        sfirst, slast = first, last
                    else:
                        sfirst, slast = (gi == 1 and j == 0), False
                    P.op('pe', 'matmul', out=pv(bS)[:, 0:qn], lhsT=ones_b.v(), rhs=pt[:, j, 0:qn], start=sfirst, stop=slast)
            if gi >= 1:
                pop_stage()
        st['g'] += len(groups)
        ng_ = len(groups)
        ts = tsum[acc]

        def stage_a():
            if SUM_MODE == 'dve':
                P.op('dve', 'tensor_tensor', out=ts[:, 0:qn], in0=ACC[acc][0][:, 0, 0:qn], in1=ACC[acc][0][:, 1, 0:qn], op=ALU.add)
                if ng_ >= 2:
                    P.op('dve', 'tensor_tensor', out=ts[:, 0:qn], in0=ts[:, 0:qn], in1=ACC[acc][1][:, 0, 0:qn], op=ALU.add)
                    P.op('dve', 'tensor_tensor', out=ts[:, 0:qn], in0=ts[:, 0:qn], in1=ACC[acc][1][:, 1, 0:qn], op=ALU.add)
            else:
                P.op('dve', 'tensor_tensor', out=tsb[acc][:, 0:qn], in0=ACC[acc][0][:, 0, 0:qn], in1=ACC[acc][0][:, 1, 0:qn], op=ALU.add)

        def stage_b():
            if SUM_MODE == 'dve':
                P.op('pe', 'matmul', out=pv(bS)[:, 0:qn], lhsT=ones_f.v(), rhs=ts[:, 0:qn], start=True, stop=True)
            else:
                P.op('pe', 'matmul', out=pv(bS)[:, 0:qn], lhsT=ones_b.v(), rhs=tsb[acc][:, 0:qn], start=(ng_ < 2), stop=True)

        if SUM_MODE != 'pe':
            defer([stage_a, None, None, stage_b, None])
        return bO, bS

    def ystore(r, h, q0, qn, ys):
        P.dma('sp', [(YT[r, :, h, q0:q0 + qn], ys[:, 0:qn])], ys)

    if units:
        load_unit(0)
    for u, (kind, h) in enumerate(units):
        s = u % 2
        if u + 1 < len(units):
            load_unit(u + 1)
        qchunks = []
        if l == 0:
            qchunks.append((0, 256, [0, 1]))
        for c in range(NLQ):
            qchunks.append((256 + 512 * c, 512, list(range(NTL))))
        for (q0, qn, kcs) in qchunks:
            if kind == 'mla':
                bO, bS = attn_pass(s, q0, qn, kcs, 0, 96, 96 ** -0.5)

                def ep_mla(bO=bO, bS=bS, q0=q0, qn=qn, h=h):
                    P.op('dve', 'reciprocal', out=rs[:, 0:qn], in_=pv(bS)[:, 0:qn])
                    ys = yst[st['y'] % 2]
                    st['y'] += 1
                    P.op('dve', 'tensor_tensor', out=ys[:, 0:qn], in0=pv(bO)[:, 0:qn], in1=rs[:, 0:qn], op=ALU.mult)
                    ystore(0, h, q0, qn, ys)
                defer([ep_mla, None])
            else:
                for m in range(2):
                    bO, bS = attn_pass(s, q0, qn, kcs, 64 * m, 64 * m + 64, 0.125)

                    def ep_d(bO=bO, bS=bS, qn=qn, m=m):
                        P.op('dve', 'reciprocal', out=rs[:, 0:qn], in_=pv(bS)[:, 0:qn])
                        P.op('dve', 'tensor_tensor', out=On[m][:, 0:qn], in0=pv(bO)[:, 0:qn], in1=rs[:, 0:qn], op=ALU.mult)
                    defer([ep_d, None])

                def ep_c1(qn=qn):
                    P.op('dve', 'scalar_tensor_tensor', out=Oc[:, 0:qn], in0=On[1][:, 0:qn], scalar=mc['nlam'][:, 0:1],
                         in1=On[0][:, 0:qn], op0=ALU.mult, op1=ALU.add)
                    P.op('dve', 'tensor_tensor', out=sqb[:, 0:qn], in0=Oc[:, 0:qn], in1=Oc[:, 0:qn], op=ALU.mult)

                def ep_c2(bS=bS, qn=qn):
                    P.op('pe', 'matmul', out=pv(bS)[:, 0:qn], lhsT=ones_f.v(), rhs=sqb[:, 0:qn], start=True, stop=True)

                def ep_c3(bS=bS, qn=qn):
                    P.op('act', 'activation', out=rstd_b[:, 0:qn], in_=pv(bS)[:, 0:qn], func=AF.Ln, scale=1.0 / 128, bias=EPS)
                    P.op('act', 'activation', out=rstd_b[:, 0:qn], in_=rstd_b[:, 0:qn], func=AF.Exp, scale=-0.5)

                def ep_c4(q0=q0, qn=qn, h=h):
                    ys = yst[st['y'] % 2]
                    st['y'] += 1
                    P.op('dve', 'scalar_tensor_tensor', out=ys[:, 0:qn], in0=Oc[:, 0:qn], scalar=mc['gcol'][:, 0:1],
                         in1=rstd_b[:, 0:qn], op0=ALU.mult, op1=ALU.mult)
                    ystore(1, h, q0, qn, ys)
                defer([ep_c1, None, ep_c2, None, ep_c3, None, ep_c4, None])
    flush_all()
    P.barrier()

    A.reset(base_off)
    Qs = A.alloc([64, 4, T], BF16, "Qs")
    Ks = A.alloc([64, T], BF16, "Ks")
    Vt = A.alloc([128, NT, 64], BF16, "Vt")
    PT = [A.alloc([128, 2, 512], BF16, f"PTs{i}") for i in range(3)]
    rs = A.alloc([64, 512], F32, "rs_s")
    yst = [A.alloc([64, 512], BF16, f"ysts{i}") for i in range(2)]
    maskL, maskR = G['maskL'], G['maskR']
    QTs, KTs, Vs = G['QTs'], G['KTs'], G['Vs']
    NLB = NTL - 2
    for g in range(4 if DBG_SWA else 0):
        P.dma('sp', [(Ks[:, 0:TL], KTs[g // 2, (g % 2) * 64:(g % 2) * 64 + 64, 0:TL])], Ks)
        P.dma('sp', [(Qs[:, j, 0:TL], QTs[(4 * g + j) // 2, ((4 * g + j) % 2) * 64:((4 * g + j) % 2) * 64 + 64, 0:TL]) for j in range(4)], Qs)
        P.dma('sp', [(Vt[:, 0:NTL, :], Vs[0:TL, g * 64:(g + 1) * 64].f(lambda a: a.rearrange("(c p) d -> p c d", p=128)))], Vt)
        blocks = []
        if l == 0:
            blocks += [(0, [(0, None), (1, None)]), (1, [(0, None), (1, None)])]
        for qb in range(NLB):
            tt = 2 + qb
            ch = [(0, None), (1, None)]
            if qb > 0:
                ch.append((tt - 1, maskL))
            ch.append((tt, None))
            if qb < NLB - 1:
                ch.append((tt + 1, maskR))
            blocks.append((tt, ch))
        for (tt, ch) in blocks:
            acc = st['acc'] % 2
            st['acc'] += 1
            bO, bS = 4 + 2 * acc, 5 + 2 * acc
            groups = [ch[i:i + 2] for i in range(0, len(ch), 2)]
            rhsq = Qs[:, :, tt * 128:(tt + 1) * 128]

            def qk(gi):
                gg = st['g'] + gi
                Sb = (gg % 2) * 2
                for j, (kc, mk) in enumerate(groups[gi]):
                    o4 = pv(Sb + j).f(lambda a: a.rearrange("p (j t) -> p j t", j=4))
                    P.op('pe', 'matmul', out=o4, lhsT=Ks[:, kc * 128:(kc + 1) * 128], rhs=rhsq, start=True, stop=(mk is None))
                    if mk is not None:
                        P.op('pe', 'matmul', out=o4, lhsT=ident_b.v(), rhs=mk.v(lambda a: bc(a, 1, [128, 4, 128])), start=False, stop=True)

            qk(0)
            for gi, grp in enumerate(groups):
                gg = st['g'] + gi
                Sb = (gg % 2) * 2
                pt = PT[gg % 3]
                if gi + 1 < len(groups):
                    qk(gi + 1)
                ng = len(grp)
                P.op('act', 'activation', out=pt[:, 0:ng, :], in_=V(ps_t[:, Sb:Sb + ng, :], PB[Sb:Sb + ng]), func=AF.Exp, scale=0.125)
                for j, (kc, mk) in enumerate(grp):
                    first = (gi == 0 and j == 0)
                    last = (gi == len(groups) - 1 and j == ng - 1)
                    P.op('pe', 'matmul', out=pv(bO)[0:64, :], lhsT=Vt[:, kc, :], rhs=pt[:, j, :], start=first, stop=last)
                    P.op('pe', 'matmul', out=pv(bS)[0:64, :], lhsT=ones_b[:, 0:64], rhs=pt[:, j, :], start=first, stop=last)
            st['g'] += len(groups)
            r3 = rs.v(lambda a: a.rearrange("p (j t) -> p j t", j=4))
            P.op('dve', 'tensor_tensor', out=r3, in0=pv(bS)[0:64, :].f(lambda a: a.rearrange("p (j t) -> p j t", j=4)),
                 in1=mc['esink'][0:64, 4 * g:4 * g + 4].f(lambda a: bc(a, 2, [64, 4, 128])), op=ALU.add)
            P.op('dve', 'reciprocal', out=rs.v(), in_=rs.v())
            ys = yst[st['y'] % 2]
            st['y'] += 1
            P.op('dve', 'tensor_tensor', out=ys.v(), in0=pv(bO)[0:64, :], in1=rs.v(), op=ALU.mult)
            P.dma('sp', [(YT[2, (j % 2) * 64:(j % 2) * 64 + 64, 2 * g + j // 2, tt * 128:(tt + 1) * 128], ys[:, j * 128:(j + 1) * 128])
                         for j in range(4)], ys)


def phase_C(nc, P, A, base_off, l, mc, G):
    pv = G['pv']
    NTL = DBG_NT or NT
    NLQ = (NTL - 2) // 4
    A.reset(base_off)
    hTc = [A.alloc([128, 8, 512], BF16, f"hTc{i}") for i in range(2)]
    Yr = [A.alloc([128, 8, 512], BF16, f"Yr{i}") for i in range(2)]
    uT = A.alloc([128, 8, 512], BF16, "uT")
    sig = A.alloc([128, 8, 512], F32, "sig")
    mm = A.alloc([128, 8, 512], F32, "mm")
    mb = A.alloc([128, 8, 512], BF16, "mb")
    xt4 = [A.alloc([128, 4, D], F32, f"xt4{i}") for i in range(2)]
    gbc = [A.alloc([128, D], F32, f"gbc{i}") for i in range(2)]
    Wr = [A.alloc([128, 8, 1024], BF16, f"Wr{i}") for i in range(3)]
    tmp = [A.alloc([128, 512], F32, f"tmpc{i}") for i in range(2)]
    junk = A.alloc([128, D], BF16, "junkc")
    ssq = A.alloc([128, 1], F32, "ssqc")
    rstd = A.alloc([128, 1], F32, "rstdc")
    fing = G['fing']
    GATE, WM, HT, YT, X1, out_d = (G[k] for k in ('GATE', 'WM', 'HT', 'YT', 'X1', 'out_d'))
    x_src = G['x_in'] if l == 0 else X1
    ctx_src = G['ctx_in'] if l == 0 else X1
    for kind in range(2):
        P.dma('sp', [(gbc[kind].v(), GATE[l, kind:kind + 1, :].f(lambda a: a.partition_broadcast(128)))], gbc[kind])

    tiles = []
    if l == 0:
        tiles.append((0, 256, 1))
    for c in range(NLQ):
        tiles.append((256 + 512 * c, 512, 0))
    units = [(ti, u) for ti in range(len(tiles)) for u in range(10)]
    wst = {'n': 0}

    def load_w(idx):
        if idx < len(units):
            ti, u = units[idx]
            P.dma('sp', [(Wr[idx % 3].v(), WM[l, u])], Wr[idx % 3])

    def load_act(ti):
        q0, qn, kind = tiles[ti]
        s = ti % 2
        nj = qn // 128
        P.dma('sp', [(hTc[s][:, :, j * 128:(j + 1) * 128], HT[q0 // 128 + j]) for j in range(nj)], hTc[s])
        for j in range(nj):
            t0 = q0 + j * 128
            if l == 0:
                src = ctx_src[t0:t0 + 128, :] if kind == 1 else x_src[t0 - 256:t0 - 128, :]
            else:
                src = X1[t0:t0 + 128, :]
            P.dma('sp', [(xt4[s][:, j, :], src)], xt4[s])

    bst = {'n': 0}

    def bank():
        b = bst['n'] % 8
        bst['n'] += 1
        return b

    tcnt = {'n': 0}
    load_w(0)
    load_w(1)
    load_act(0)
    ycnt = 0
    for ti, (q0, qn, kind) in enumerate(tiles):
        s = ti % 2
        nj = qn // 128
        if ti + 1 < len(tiles):
            load_act(ti + 1)
        h = hTc[s]
        for r in range(3):
            yr = Yr[ycnt % 2]
            ycnt += 1
            P.dma('sp', [(yr[:, :, 0:qn], YT[r, :, :, q0:q0 + qn])], yr)
            idx = ti * 10 + 3 * r
            load_w(idx + 2)
            Wz = Wr[idx % 3]
            for cc in range(8):
                b = bank()
                for k in range(8):
                    P.op('pe', 'matmul', out=pv(b)[:, 0:qn], lhsT=Wz[:, k, cc * 128:(cc + 1) * 128], rhs=h[:, k, 0:qn],
                         start=(k == 0), stop=(k == 7))
                tp = tmp[tcnt['n'] % 2]
                tcnt['n'] += 1
                P.op('act', 'activation', out=tp[:, 0:qn], in_=pv(b)[:, 0:qn], func=AF.Silu)
                P.op('dve', 'tensor_tensor', out=uT[:, cc, 0:qn], in0=tp[:, 0:qn], in1=yr[:, cc, 0:qn], op=ALU.mult)
            idx += 1
            load_w(idx + 2)
            Wg = Wr[idx % 3]
            for dc in range(8):
                b = bank()
                for k in range(8):
                    P.op('pe', 'matmul', out=pv(b)[:, 0:qn], lhsT=Wg[:, k, dc * 128:(dc + 1) * 128], rhs=h[:, k, 0:qn],
                         start=(k == 0), stop=(k == 7))
                P.op('act', 'activation', out=sig[:, dc, 0:qn], in_=pv(b)[:, 0:qn], func=AF.Sigmoid)
            idx += 1
            load_w(idx + 2)
            Wb = Wr[idx % 3]
            for dc in range(8):
                b = bank()
                for k in range(8):
                    P.op('pe', 'matmul', out=pv(b)[:, 0:qn], lhsT=Wb[:, k, dc * 128:(dc + 1) * 128], rhs=uT[:, k, 0:qn],
                         start=(k == 0), stop=(k == 7))
                if r == 0:
                    P.op('dve', 'tensor_tensor', out=mm[:, dc, 0:qn], in0=pv(b)[:, 0:qn], in1=sig[:, dc, 0:qn], op=ALU.mult)
                else:
                    tp = tmp[tcnt['n'] % 2]
                    tcnt['n'] += 1
                    P.op('dve', 'tensor_tensor', out=tp[:, 0:qn], in0=pv(b)[:, 0:qn], in1=sig[:, dc, 0:qn], op=ALU.mult)
                    dst = mm if r == 1 else mb
                    P.op('dve', 'tensor_tensor', out=dst[:, dc, 0:qn], in0=mm[:, dc, 0:qn], in1=tp[:, 0:qn], op=ALU.add)
        idx = ti * 10 + 9
        load_w(idx + 2)
        Wo = Wr[idx % 3]
        xs = xt4[s]
        for j in range(nj):
            for og in range(2):
                b = bank()
                for k in range(8):
                    P.op('pe', 'matmul', out=pv(b), lhsT=mb[:, k, j * 128:(j + 1) * 128], rhs=Wo[:, k, og * 512:(og + 1) * 512],
                         start=(k == 0), stop=(k == 7))
                tp = tmp[tcnt['n'] % 2]
                tcnt['n'] += 1
                P.op('dve', 'tensor_tensor', out=tp.v(), in0=pv(b), in1=gbc[kind][:, og * 512:(og + 1) * 512], op=ALU.mult)
                P.op('dve', 'tensor_tensor', out=xs[:, j, og * 512:(og + 1) * 512], in0=tp.v(), in1=xs[:, j, og * 512:(og + 1) * 512], op=ALU.add)
            t0 = q0 + j * 128
            if l < DEPTH - 1:
                P.dma('sp', [(X1[t0:t0 + 128, :], xs[:, j, :])], xs)
            else:
                P.op('act', 'activation', out=junk.v(), in_=xs[:, j, :], func=AF.Square, accum_out=ssq.v())
                P.op('act', 'activation', out=ssq.v(), in_=ssq.v(), func=AF.Sqrt, scale=1.0 / D, bias=EPS)
                P.op('dve', 'reciprocal', out=rstd.v(), in_=ssq.v())
                P.op('dve', 'scalar_tensor_tensor', out=xs[:, j, :], in0=xs[:, j, :], scalar=rstd[:, 0:1], in1=fing.v(),
                     op0=ALU.mult, op1=ALU.mult)
                P.dma('sp', [(out_d[t0 - 256:t0 - 128, :], xs[:, j, :])], xs)


def host_consts():
    ident = np.eye(128, dtype=np.float32)
    jj = np.arange(128)[:, None]
    ii = np.arange(128)[None, :]
    maskL = np.where(jj >= ii, 0.0, -30000.0).astype(np.float32)
    maskR = np.where(jj <= ii, 0.0, -30000.0).astype(np.float32)
    t = np.arange(SEQ)
    row = (t // 64).astype(np.float32)
    col = (t % 64).astype(np.float32)

    def tables(rot):
        nf = rot // 4
        inv = (np.float32(10000.0) ** (-np.arange(nf, dtype=np.float32) / np.float32(nf))).astype(np.float32)
        ang = np.concatenate([row[:, None] * inv, col[:, None] * inv], axis=-1).astype(np.float32)
        return np.cos(ang).astype(np.float32), np.sin(ang).astype(np.float32)

    rope = np.zeros((T, 192), np.float32)
    rope[:CTX, 0:64] = 1.0
    rope[:CTX, 128:160] = 1.0
    c, s = tables(64)
    rope[CTX:, 0:64] = np.concatenate([c, c], -1)
    rope[CTX:, 64:128] = np.concatenate([-s, s], -1)
    c, s = tables(32)
    rope[CTX:, 128:160] = np.concatenate([c, c], -1)
    rope[CTX:, 160:192] = np.concatenate([-s, s], -1)
    return ident, maskL, maskR, rope


def make_in_maps(inp, cores):
    f = lambda a: np.ascontiguousarray(np.asarray(a, dtype=np.float32))
    ident, maskL, maskR, rope = host_consts()
    colT = lambda v, n: f(np.asarray(v).reshape(n, 128).T)
    shared = dict(
        bmodT=f(np.stack([colT(inp['b_mod'][l], 24) for l in range(DEPTH)])),
        ngT=f(np.stack([colT(inp['norm_g'][l], 8) for l in range(DEPTH)])),
        qnT=f(np.stack([colT(inp['mla_q_norm'][l], 3) for l in range(DEPTH)])),
        kvnT=f(np.stack([colT(inp['mla_kv_norm'][l], 2) for l in range(DEPTH)])),
        lqk=f(np.stack([np.concatenate([np.asarray(inp[k][l]) for k in ('diff_lq1', 'diff_lk1', 'diff_lq2', 'diff_lk2')])[None, :]
                        for l in range(DEPTH)])),
        sublnT=f(np.asarray(inp['diff_subln']).reshape(DEPTH, 128, 1)),
        sink=f(np.asarray(inp['swa_sink']).reshape(DEPTH, 1, 16)),
        fing=f(np.asarray(inp['final_norm']).reshape(1, D)),
        w_mod=f(inp['w_mod']), w_in=f(inp['w_in']), mla_w_uq=f(inp['mla_w_uq']), mla_w_ukv=f(inp['mla_w_ukv']),
        w_branch=f(inp['w_branch']), w_out=f(inp['w_out']),
        ident=ident, maskL=maskL, maskR=maskR, rope=rope,
    )
    x = np.asarray(inp['x'])
    c = np.asarray(inp['c'])
    ctx = np.asarray(inp['ctx'])
    cc = np.asarray(inp['c_ctx'])
    maps = []
    for b in cores:
        m = dict(shared)
        m['x_b'] = f(x[b])
        m['ctx_b'] = f(ctx[b])
        c2 = np.stack([np.asarray(c[b]).reshape(8, 128).T, cc.reshape(8, 128).T], axis=-1)
        m['c2'] = f(c2)
        maps.append(m)
    return maps


def kernel(**inputs):
    nc = build_program()
    maps = make_in_maps(inputs, list(range(NCORES)))
    res = run_bass_kernel_spmd(nc, maps, core_ids=list(range(NCORES)))
    out = np.stack([np.asarray(r["out"], dtype=np.float32) for r in res.results], axis=0)
    return out
```
